# Optimizing a Trainium2 kernel written in Bass

```python
import math
import jax, jax.numpy as jnp
from jax import lax
import numpy as np

D_MODEL = 2048
BATCH = 1
SEQ = 16384
DEPTH = 2
DEC_BATCH = 32
DEC_SEQ = 64
PAST_LEN = 2048

CHUNK = 64
MIX_WIDTH = D_MODEL
SSD_HEAD_DIM = 64
SSD_WIDTH = MIX_WIDTH // 2
SSD_HEADS = SSD_WIDTH // SSD_HEAD_DIM
SSD_GROUPS = 2
SSD_STATE = 128
SSD_CONV = 4
SSD_BLOCK = CHUNK
SSD_CONV_DIM = SSD_WIDTH + 2 * SSD_GROUPS * SSD_STATE
DIFF_HD = 64
DIFF_VD = 2 * DIFF_HD
DIFF_WIDTH = MIX_WIDTH // 4
DIFF_HEADS = DIFF_WIDTH // DIFF_VD
ATTN_QBLOCK = 128
GLA_DK = 64
GLA_DV = 128
GLA_WIDTH = MIX_WIDTH // 4
GLA_HEADS = GLA_WIDTH // GLA_DV
GLA_GATE_RANK = 16
GLA_TAU = 16.0
GLA_BLOCK = 16
D_FF = 4 * D_MODEL
EPS = 1e-6

IN_SPLITS = (SSD_WIDTH, SSD_CONV_DIM, SSD_HEADS,
             DIFF_HEADS * 2 * DIFF_HD, DIFF_HEADS * 2 * DIFF_HD, DIFF_WIDTH,
             GLA_HEADS * GLA_DK, GLA_HEADS * GLA_DK, GLA_WIDTH, GLA_GATE_RANK, GLA_WIDTH)
IN_COLS = sum(IN_SPLITS)

kernel_name = 'hymba_ssd_diffattn_gla_stream_step'


def _rms(x):
    xf = x.astype(jnp.float32)
    return xf * lax.rsqrt(jnp.mean(xf * xf, axis=-1, keepdims=True) + EPS)


def _pad_time(a, pad):
    if pad == 0:
        return a
    widths = [(0, 0)] * a.ndim
    widths[1] = (0, pad)
    return jnp.pad(a, widths)


def _carry_chunks(h0, decay, states):
    def step(h, inp):
        d, s = inp
        return d * h + s, h
    h_last, h_in = lax.scan(step, h0, (jnp.moveaxis(decay, 1, 0), jnp.moveaxis(states, 1, 0)))
    return jnp.moveaxis(h_in, 0, 1), h_last


def _ssd_scan(x, dt, a, bm, cm, h0):
    bsz, L = x.shape[:2]
    hg = SSD_HEADS // SSD_GROUPS
    q = SSD_BLOCK
    pad = (-L) % q
    x, dt, bm, cm = (_pad_time(t, pad) for t in (x, dt, bm, cm))
    nc = (L + pad) // q
    x = x.reshape(bsz, nc, q, SSD_GROUPS, hg, SSD_HEAD_DIM)
    dt = dt.reshape(bsz, nc, q, SSD_GROUPS, hg)
    bm = bm.reshape(bsz, nc, q, SSD_GROUPS, SSD_STATE)
    cm = cm.reshape(bsz, nc, q, SSD_GROUPS, SSD_STATE)
    cs = jnp.cumsum(dt * a.reshape(SSD_GROUPS, hg), axis=2)
    xdt = x * dt[..., None]
    causal = jnp.tril(jnp.ones((q, q), bool))[:, :, None, None]
    seg = cs[:, :, :, None] - cs[:, :, None]
    lmat = jnp.exp(jnp.where(causal, seg, -jnp.inf))
    cb = jnp.einsum('bctgn,bcsgn->bctsg', cm, bm)
    y_diag = jnp.einsum('bctsgh,bcsghp->bctghp', cb[..., None] * lmat, xdt)
    xdt_end = xdt * jnp.exp(cs[:, :, -1:] - cs)[..., None]
    states = jnp.einsum('bcsgn,bcsghp->bcghpn', bm, xdt_end)
    h0g = h0.reshape(bsz, SSD_GROUPS, hg, SSD_HEAD_DIM, SSD_STATE)
    h_in, h_last = _carry_chunks(h0g, jnp.exp(cs[:, :, -1])[..., None, None], states)
    y_off = jnp.einsum('bctgn,bcghpn->bctghp', cm, h_in) * jnp.exp(cs)[..., None]
    y = (y_diag + y_off).reshape(bsz, nc * q, SSD_HEADS, SSD_HEAD_DIM)[:, :L]
    return y, h_last.reshape(bsz, SSD_HEADS, SSD_HEAD_DIM, SSD_STATE)


def _gla_scan(q, k, v, log_a, s0):
    bsz, L = q.shape[:2]
    c = GLA_BLOCK
    pad = (-L) % c
    q, k, v, log_a = (_pad_time(t, pad) for t in (q, k, v, log_a))
    nc = (L + pad) // c
    q = q.reshape(bsz, nc, c, GLA_HEADS, GLA_DK)
    k = k.reshape(bsz, nc, c, GLA_HEADS, GLA_DK)
    log_a = log_a.reshape(bsz, nc, c, GLA_HEADS, GLA_DK)
    v = v.reshape(bsz, nc, c, GLA_HEADS, GLA_DV)
    b = jnp.cumsum(log_a, axis=2)
    qt = q * jnp.exp(b)
    kt = k * jnp.exp(-b)
    causal = jnp.tril(jnp.ones((c, c), bool))
    att = jnp.where(causal, jnp.einsum('bcthk,bcshk->bchts', qt, kt), 0.0)
    o = jnp.einsum('bchts,bcshv->bcthv', att, v)
    states = jnp.einsum('bcshk,bcshv->bchkv', k * jnp.exp(b[:, :, -1:] - b), v)
    s_in, s_last = _carry_chunks(s0, jnp.exp(b[:, :, -1])[..., None], states)
    o = o + jnp.einsum('bcthk,bchkv->bcthv', qt, s_in)
    return o.reshape(bsz, nc * c, GLA_HEADS, GLA_DV)[:, :L], s_last


def _causal_conv(u, prev, w, bias):
    L = u.shape[1]
    full = jnp.concatenate([prev.astype(u.dtype), u], axis=1)
    y = bias + full[:, 0:L] * w[0]
    for j in range(1, SSD_CONV):
        y = y + full[:, j:j + L] * w[j]
    return jax.nn.silu(y), full[:, -(SSD_CONV - 1):]


def _diff_core(q, k, v, q_pos, k_pos, lam, slopes):
    s = jnp.einsum('bqhcd,bkhcd->bhcqk', q, k) * (DIFF_HD ** -0.5)
    dist = jnp.abs(q_pos[:, None] - k_pos[None, :]).astype(jnp.float32)
    s = s - slopes[None, :, None, None, None] * dist
    allowed = (k_pos[None, :] // CHUNK) <= (q_pos[:, None] // CHUNK)
    p = jax.nn.softmax(jnp.where(allowed, s, -jnp.inf), axis=-1)
    a = p[:, :, 0] - lam * p[:, :, 1]
    return jnp.einsum('bhqk,bkhv->bqhv', a, v)


def _diff_attention(q, k, v, q_pos, k_pos, lam, slopes):
    bsz, lq = q.shape[:2]
    if lq <= ATTN_QBLOCK:
        return _diff_core(q, k, v, q_pos, k_pos, lam, slopes)
    nb = lq // ATTN_QBLOCK
    qb = q.reshape(bsz, nb, ATTN_QBLOCK, DIFF_HEADS, 2, DIFF_HD).swapaxes(0, 1)
    pb = q_pos.reshape(nb, ATTN_QBLOCK)
    out = lax.map(lambda t: _diff_core(t[0], k, v, t[1], k_pos, lam, slopes), (qb, pb))
    return out.swapaxes(0, 1).reshape(bsz, lq, DIFF_HEADS, DIFF_VD)


def _layer(x, q_pos, k_past, v_past, conv_prev, h_prev, s_prev, layer_idx,
           norm1_g, w_in, ssd_conv_w, ssd_conv_b, ssd_dt_bias, ssd_a_log, ssd_d, ssd_norm_g,
           diff_qn_g, diff_kn_g, diff_lambda, diff_out_g, gla_wa2, gla_ba, gla_norm_g,
           w_out, norm2_g, w_mlp1, w_mlp2):
    f32 = jnp.float32
    bsz, L, _ = x.shape
    dty = x.dtype
    h = (_rms(x) * norm1_g.astype(f32)).astype(dty)
    proj = h @ w_in
    z, xbc, dt_raw, dq, dk, dv, gq, gk, gv, ga, gg = jnp.split(
        proj, np.cumsum(IN_SPLITS)[:-1].tolist(), axis=-1)

    xbc, conv_new = _causal_conv(xbc, conv_prev, ssd_conv_w, ssd_conv_b)
    xs, bm, cm = jnp.split(xbc, [SSD_WIDTH, SSD_WIDTH + SSD_GROUPS * SSD_STATE], axis=-1)
    xs_h = xs.astype(f32).reshape(bsz, L, SSD_HEADS, SSD_HEAD_DIM)
    dt = jax.nn.softplus(dt_raw.astype(f32) + ssd_dt_bias.astype(f32))
    a = -jnp.exp(ssd_a_log.astype(f32))
    y, h_new = _ssd_scan(xs_h, dt, a,
                         bm.astype(f32).reshape(bsz, L, SSD_GROUPS, SSD_STATE),
                         cm.astype(f32).reshape(bsz, L, SSD_GROUPS, SSD_STATE),
                         h_prev.astype(f32))
    y = y + ssd_d.astype(f32)[:, None] * xs_h
    y = y.reshape(bsz, L, SSD_WIDTH) * jax.nn.silu(z.astype(f32))
    y = _rms(y.reshape(bsz, L, SSD_GROUPS, SSD_WIDTH // SSD_GROUPS)).reshape(bsz, L, SSD_WIDTH)
    y = y * ssd_norm_g.astype(f32)

    q = _rms(dq.reshape(bsz, L, DIFF_HEADS, 2, DIFF_HD)) * diff_qn_g.astype(f32)
    k = _rms(dk.reshape(bsz, L, DIFF_HEADS, 2, DIFF_HD)) * diff_kn_g.astype(f32)
    k_rows = k.astype(dty).reshape(bsz, L, DIFF_HEADS, 2 * DIFF_HD)
    v_rows = dv.reshape(bsz, L, DIFF_HEADS, DIFF_VD)
    if k_past is None:
        k_all, v_all, k_pos = k_rows, v_rows, q_pos
    else:
        k_all = jnp.concatenate([k_past.astype(dty), k_rows], axis=1)
        v_all = jnp.concatenate([v_past.astype(dty), v_rows], axis=1)
        k_pos = jnp.arange(k_past.shape[1] + L)
    lam_init = 0.8 - 0.6 * math.exp(-0.3 * layer_idx)
    lq1, lk1, lq2, lk2 = diff_lambda.astype(f32)
    lam = jnp.exp(jnp.sum(lq1 * lk1)) - jnp.exp(jnp.sum(lq2 * lk2)) + lam_init
    slopes = jnp.exp2(-8.0 * jnp.arange(1, DIFF_HEADS + 1, dtype=f32) / DIFF_HEADS)
    o = _diff_attention(q, k_all.astype(f32).reshape(bsz, -1, DIFF_HEADS, 2, DIFF_HD),
                        v_all.astype(f32), q_pos, k_pos, lam, slopes)
    o = (_rms(o) * diff_out_g.astype(f32) * (1.0 - lam_init)).reshape(bsz, L, DIFF_WIDTH)

    gq_h = gq.astype(f32).reshape(bsz, L, GLA_HEADS, GLA_DK) * (GLA_DK ** -0.5)
    gk_h = gk.astype(f32).reshape(bsz, L, GLA_HEADS, GLA_DK)
    gv_h = gv.astype(f32).reshape(bsz, L, GLA_HEADS, GLA_DV)
    log_a = jax.nn.log_sigmoid((ga @ gla_wa2).astype(f32) + gla_ba.astype(f32)) / GLA_TAU
    o_g, s_new = _gla_scan(gq_h, gk_h, gv_h, log_a.reshape(bsz, L, GLA_HEADS, GLA_DK),
                           s_prev.astype(f32))
    o_g = (_rms(o_g) * gla_norm_g.astype(f32)).reshape(bsz, L, GLA_WIDTH) * jax.nn.silu(gg.astype(f32))

    x = x + jnp.concatenate([y, o, o_g], axis=-1).astype(dty) @ w_out
    h2 = (_rms(x) * norm2_g.astype(f32)).astype(dty)
    x = x + jnp.square(jax.nn.relu(h2 @ w_mlp1)) @ w_mlp2
    return x, k_rows, v_rows, conv_new, h_new.astype(dty), s_new.astype(dty)


def setup_inputs(seed: int = 0) -> dict:
    key = jax.random.key(seed)
    ks = jax.random.split(key, 32)
    f32 = jnp.float32
    nrm = lambda k, shape, s: jax.random.normal(k, shape, f32) * s
    dt0 = jnp.exp(jax.random.uniform(ks[9], (DEPTH, SSD_HEADS), f32) * (math.log(0.1) - math.log(1e-3)) + math.log(1e-3))
    return {
        'x_prompt': nrm(ks[0], (BATCH, SEQ, D_MODEL), 1.0),
        'x_sample': nrm(ks[1], (DEC_BATCH, DEC_SEQ, D_MODEL), 1.0),
        'cache_diff_k': nrm(ks[2], (DEPTH, DEC_BATCH, PAST_LEN, DIFF_HEADS, 2 * DIFF_HD), 1.0),
        'cache_diff_v': nrm(ks[3], (DEPTH, DEC_BATCH, PAST_LEN, DIFF_HEADS, DIFF_VD), 1.0),
        'state_ssd_conv': nrm(ks[4], (DEPTH, DEC_BATCH, SSD_CONV - 1, SSD_CONV_DIM), 1.0),
        'state_ssd': nrm(ks[5], (DEPTH, DEC_BATCH, SSD_HEADS, SSD_HEAD_DIM, SSD_STATE), 0.1),
        'state_gla': nrm(ks[6], (DEPTH, DEC_BATCH, GLA_HEADS, GLA_DK, GLA_DV), 0.5),
        'norm1_g': 1.0 + nrm(ks[7], (DEPTH, D_MODEL), 0.02),
        'w_in': nrm(ks[8], (DEPTH, D_MODEL, IN_COLS), D_MODEL ** -0.5),
        'ssd_conv_w': nrm(ks[10], (DEPTH, SSD_CONV, SSD_CONV_DIM), SSD_CONV ** -0.5),
        'ssd_conv_b': nrm(ks[11], (DEPTH, SSD_CONV_DIM), 0.01),
        'ssd_dt_bias': dt0 + jnp.log(-jnp.expm1(-dt0)),
        'ssd_a_log': jnp.log(jax.random.uniform(ks[12], (DEPTH, SSD_HEADS), f32, 1.0, 16.0)),
        'ssd_d': 1.0 + nrm(ks[13], (DEPTH, SSD_HEADS), 0.1),
        'ssd_norm_g': 1.0 + nrm(ks[14], (DEPTH, SSD_WIDTH), 0.02),
        'diff_qn_g': 1.0 + nrm(ks[15], (DEPTH, DIFF_HD), 0.02),
        'diff_kn_g': 1.0 + nrm(ks[16], (DEPTH, DIFF_HD), 0.02),
        'diff_lambda': nrm(ks[17], (DEPTH, 4, DIFF_HD), 0.1),
        'diff_out_g': 1.0 + nrm(ks[18], (DEPTH, DIFF_VD), 0.02),
        'gla_wa2': nrm(ks[19], (DEPTH, GLA_GATE_RANK, GLA_HEADS * GLA_DK), GLA_GATE_RANK ** -0.5),
        'gla_ba': nrm(ks[20], (DEPTH, GLA_HEADS * GLA_DK), 0.01),
        'gla_norm_g': 1.0 + nrm(ks[21], (DEPTH, GLA_DV), 0.02),
        'w_out': nrm(ks[22], (DEPTH, MIX_WIDTH, D_MODEL), MIX_WIDTH ** -0.5),
        'norm2_g': 1.0 + nrm(ks[23], (DEPTH, D_MODEL), 0.02),
        'w_mlp1': nrm(ks[24], (DEPTH, D_MODEL, D_FF), D_MODEL ** -0.5),
        'w_mlp2': nrm(ks[25], (DEPTH, D_FF, D_MODEL), D_FF ** -0.5),
    }


def reference(x_prompt, x_sample, cache_diff_k, cache_diff_v, state_ssd_conv, state_ssd, state_gla,
              norm1_g, w_in, ssd_conv_w, ssd_conv_b, ssd_dt_bias, ssd_a_log, ssd_d, ssd_norm_g,
              diff_qn_g, diff_kn_g, diff_lambda, diff_out_g, gla_wa2, gla_ba, gla_norm_g,
              w_out, norm2_g, w_mlp1, w_mlp2):
    def params(l):
        return (norm1_g[l], w_in[l], ssd_conv_w[l], ssd_conv_b[l], ssd_dt_bias[l], ssd_a_log[l],
                ssd_d[l], ssd_norm_g[l], diff_qn_g[l], diff_kn_g[l], diff_lambda[l], diff_out_g[l],
                gla_wa2[l], gla_ba[l], gla_norm_g[l], w_out[l], norm2_g[l], w_mlp1[l], w_mlp2[l])

    bp, lp = x_prompt.shape[:2]
    dty = x_prompt.dtype
    pos_p = jnp.arange(lp)
    conv0 = jnp.zeros((bp, SSD_CONV - 1, SSD_CONV_DIM), dty)
    h0 = jnp.zeros((bp, SSD_HEADS, SSD_HEAD_DIM, SSD_STATE), dty)
    s0 = jnp.zeros((bp, GLA_HEADS, GLA_DK, GLA_DV), dty)
    y_p = x_prompt
    outs_p = []
    for l in range(DEPTH):
        y_p, *st = _layer(y_p, pos_p, None, None, conv0, h0, s0, l, *params(l))
        outs_p.append(st)
    kp, vp, cp, hp, sp = (jnp.stack(f) for f in zip(*outs_p))

    ls = x_sample.shape[1]
    pos_s = cache_diff_k.shape[2] + jnp.arange(ls)
    y_s = x_sample
    outs_s = []
    for l in range(DEPTH):
        y_s, *st = _layer(y_s, pos_s, cache_diff_k[l], cache_diff_v[l], state_ssd_conv[l],
                          state_ssd[l], state_gla[l], l, *params(l))
        outs_s.append(st)
    k_s, v_s, c_s, h_s, s_s = (jnp.stack(f) for f in zip(*outs_s))

    return (y_p, y_s, kp, vp, cp, hp, sp, k_s, v_s, c_s, h_s, s_s)
```

```python
import math
import os
from contextlib import ExitStack
from types import SimpleNamespace

import numpy as np
import ml_dtypes
import concourse.bass as bass
import concourse.mybir as mybir
from concourse.bass_utils import run_bass_kernel_spmd

F32 = mybir.dt.float32
BF16 = mybir.dt.bfloat16
AF = mybir.ActivationFunctionType
ALU = mybir.AluOpType
AX = mybir.AxisListType
EPS = 1e-6
NEG = -30000.0


def make_cfg(D=2048, SEQ=16384, DEC_B=32, DEC_S=64, PAST=2048, DEPTH=2, NCORE=8):
    c = SimpleNamespace()
    c.D, c.SEQ, c.DEC_B, c.DEC_S, c.PAST, c.DEPTH, c.NCORE = D, SEQ, DEC_B, DEC_S, PAST, DEPTH, NCORE
    c.KC = D // 128
    c.SW = D // 2
    c.SH = c.SW // 64
    c.HG = c.SH // 2
    c.CONVD = c.SW + 512
    c.DH = (D // 4) // 128
    c.GH = (D // 4) // 128
    c.GW = D // 4
    c.DFF = 4 * D
    sp = (c.SW, c.CONVD, c.SH, c.DH * 128, c.DH * 128, D // 4, c.GH * 64, c.GH * 64, D // 4, 16, D // 4)
    offs = np.concatenate([[0], np.cumsum(sp)]).tolist()
    (c.oZ, c.oXBC, c.oDT, c.oDQ, c.oDK, c.oDV, c.oGQ, c.oGK, c.oGV, c.oGA, c.oGG, c.INC) = offs
    c.TOKP = SEQ // NCORE
    c.NS = DEC_B // NCORE
    c.NCHP = c.TOKP // 64
    c.NCH = c.NCHP + c.NS
    c.NTOK = c.NCH * 64
    c.NT = c.NTOK // 128
    c.NTP = c.TOKP // 128
    c.NKT = c.TOKP // 128
    c.NPT = PAST // 128
    assert c.TOKP % 128 == 0 and c.NS % 2 == 0 and PAST % 128 == 0
    return c


class Buf:
    __slots__ = ("name", "w", "r", "is_out", "psum")

    def __init__(self, name):
        self.name = name
        self.w = {}
        self.r = {}
        self.is_out = False
        self.psum = False


class V:
    __slots__ = ("ap", "buf")

    def __init__(self, ap, buf):
        self.ap = ap
        self.buf = buf

    def __getitem__(self, idx):
        return V(self.ap[idx], self.buf)

    def bitcast(self, dt):
        return V(self.ap.bitcast(dt), self.buf)

    def rr(self, pat, **kw):
        return V(self.ap.rearrange(pat, **kw), self.buf)

    def bc(self, shape):
        return V(self.ap.to_broadcast(list(shape)), self.buf)

    def unsq(self, k):
        return V(self.ap.unsqueeze(k), self.buf)

    def on(self, buf):
        return V(self.ap, buf)


class Eng:
    def __init__(self, K, name, eng, is_dma_queue=False):
        self.K = K
        self.name = name
        self.eng = eng
        self.sid = K.new_sem(name)
        self.n = 0
        self.clock = {}


class Pool:
    def __init__(self, K, name, shape, dtype, n, space="sbuf"):
        self.items = []
        for i in range(n):
            self.items.append(K.tile(f"{name}{i}", shape, dtype, space))
        self.i = 0

    def next(self):
        t = self.items[self.i % len(self.items)]
        self.i += 1
        return t


class Builder:
    NDMA_SP = 24
    NDMA_PL = 24

    def __init__(self, nc):
        self.nc = nc
        self.stack = ExitStack()
        self.sems = []
        self.evclock = {}
        self.pe = Eng(self, "pe", nc.tensor)
        self.dve = Eng(self, "dve", nc.vector)
        self.act = Eng(self, "act", nc.scalar)
        self.pool = Eng(self, "pool", nc.gpsimd)
        self.sp = Eng(self, "sp", nc.sync)
        self.dsems = {}
        for q, n in ((self.sp, self.NDMA_SP), (self.pool, self.NDMA_PL)):
            self.dsems[q.name] = [[self.new_sem(f"d{q.name}{i}"), 0] for i in range(n)]
        self.dcount = {self.sp.name: 0, self.pool.name: 0}
        self.ccsem = self.new_sem("cc")
        self.ccval = 0
        self.out_events = []
        self.nwaits = 0
        self.fence = {}
        self.scope_bufs = [[]]

    def new_sem(self, name):
        h = self.stack.enter_context(self.nc.semaphore(name))
        self.sems.append(h)
        return len(self.sems) - 1

    def tile(self, name, shape, dtype, space="sbuf"):
        if space == "sbuf":
            t = self.stack.enter_context(self.nc.sbuf_tensor(name, list(shape), dtype))
        else:
            t = self.stack.enter_context(self.nc.psum_tensor(name, list(shape), dtype))
        b = Buf(name)
        b.psum = (space != "sbuf")
        b.w = dict(self.fence)
        self.scope_bufs[-1].append(b)
        return V(t[:], b)

    def dram(self, name, shape, dtype, kind=None):
        if kind is None:
            t = self.nc.dram_tensor(name, list(shape), dtype)
        else:
            t = self.nc.dram_tensor(name, list(shape), dtype, kind=kind)
        b = Buf(name)
        b.is_out = (kind == "ExternalOutput")
        return V(t.ap(), b)

    def _wait(self, E, deps):
        for sid, v in deps.items():
            if E.clock.get(sid, 0) >= v:
                continue
            E.eng.wait_ge(self.sems[sid], v)
            self.nwaits += 1
            c = dict(E.clock)
            c[sid] = v
            oc = self.evclock.get((sid, v))
            if oc:
                for k2, v2 in oc.items():
                    if c.get(k2, 0) < v2:
                        c[k2] = v2
            E.clock = c

    def _deps(self, E, reads, writes, own_sid):
        deps = {}
        for b in reads:
            for sid, v in b.w.items():
                if deps.get(sid, 0) < v:
                    deps[sid] = v
            if b.psum:
                for sid, v in b.r.items():
                    if sid != own_sid and deps.get(sid, 0) < v:
                        deps[sid] = v
        for b in writes:
            for sid, v in b.r.items():
                if sid != own_sid and deps.get(sid, 0) < v:
                    deps[sid] = v
            for sid, v in b.w.items():
                if sid != own_sid and deps.get(sid, 0) < v:
                    deps[sid] = v
        return deps

    def _record(self, ev, reads, writes):
        sid, v = ev
        for b in reads:
            b.r[sid] = v
        for b in writes:
            if b.r:
                b.r = {}
                b.w = {}
            b.w[sid] = v

    def op(self, E, fn, reads, writes):
        self._wait(E, self._deps(E, reads, writes, E.sid))
        inst = fn()
        E.n += 1
        inst.then_inc(self.sems[E.sid], 1)
        ev = (E.sid, E.n)
        self.evclock[ev] = E.clock
        self._record(ev, reads, writes)
        return ev

    def dma(self, Q, out, in_, is_output=False, slow=False):
        lst = self.dsems[Q.name]
        slot = lst[self.dcount[Q.name] % len(lst)]
        self.dcount[Q.name] += 1
        sid, prev = slot
        deps = self._deps(Q, [in_.buf], [out.buf], -1)
        if prev > 0:
            deps[sid] = max(deps.get(sid, 0), prev)
        self._wait(Q, deps)
        if slow:
            Q.eng.dma_start(out=out.ap, in_=in_.ap, allow_slow_non_contiguous=True).then_inc(self.sems[sid], 16)
        else:
            Q.eng.dma_start(out=out.ap, in_=in_.ap).then_inc(self.sems[sid], 16)
        slot[1] = prev + 16
        ev = (sid, prev + 16)
        self.evclock[ev] = Q.clock
        self._record(ev, [in_.buf], [out.buf])
        if is_output or out.buf.is_out:
            self.out_events.append(ev)
        return ev

    def allgather(self, out, in_, ncore):
        if os.environ.get("NO_CC"):
            rows = in_.ap.shape[0]
            for r in range(ncore):
                self.dma(self.sp, out[r * rows:(r + 1) * rows, :], in_)
            return
        Q = self.pool
        deps = self._deps(Q, [in_.buf], [out.buf], -1)
        if self.ccval > 0:
            deps[self.ccsem] = self.ccval
        self._wait(Q, deps)
        Q.eng.collective_compute("AllGather", ALU.bypass, replica_groups=[list(range(ncore))],
                                 ins=[in_.ap.opt()], outs=[out.ap.opt()]).then_inc(self.sems[self.ccsem], 1)
        self.ccval += 1
        ev = (self.ccsem, self.ccval)
        self.evclock[ev] = Q.clock
        self._record(ev, [in_.buf], [out.buf])

    def finish(self):
        deps = {}
        for E in (self.pe, self.dve, self.act, self.pool):
            if E.n > 0:
                deps[E.sid] = E.n
        for lst in self.dsems.values():
            for sid, v in lst:
                if v > 0:
                    deps[sid] = v
        if self.ccval > 0:
            deps[self.ccsem] = self.ccval
        self._wait(self.sp, deps)

    def mm(self, out, lhsT, rhs, start=True, stop=True):
        return self.op(self.pe, lambda: self.nc.tensor.matmul(out.ap, lhsT.ap, rhs.ap, start=start, stop=stop,
                                                             skip_group_check=True),
                       [lhsT.buf, rhs.buf], [out.buf])

    def tr(self, out, in_, ident):
        return self.op(self.pe, lambda: self.nc.tensor.transpose(out.ap, in_.ap, ident.ap),
                       [in_.buf, ident.buf], [out.buf])

    def actf(self, out, in_, func, bias=None, scale=1.0, accum=None):
        reads = [in_.buf]
        kw = {}
        if bias is not None:
            if isinstance(bias, V):
                reads.append(bias.buf)
                kw["bias"] = bias.ap
            else:
                kw["bias"] = bias
        if isinstance(scale, V):
            reads.append(scale.buf)
            kw["scale"] = scale.ap
        else:
            kw["scale"] = scale
        writes = [out.buf]
        if accum is not None:
            kw["accum_out"] = accum.ap
            writes.append(accum.buf)
        return self.op(self.act, lambda: self.nc.scalar.activation(out=out.ap, in_=in_.ap, func=func, **kw),
                       reads, writes)

    def _eng(self, E):
        return E.eng

    def tt(self, E, out, in0, in1, op):
        return self.op(E, lambda: E.eng.tensor_tensor(out=out.ap, in0=in0.ap, in1=in1.ap, op=op),
                       [in0.buf, in1.buf], [out.buf])

    def ts(self, E, out, in0, s1, s2=None, op0=ALU.mult, op1=None, accum=None):
        reads = [in0.buf]
        a1 = s1
        a2 = s2
        if isinstance(s1, V):
            reads.append(s1.buf)
            a1 = s1.ap
        if isinstance(s2, V):
            reads.append(s2.buf)
            a2 = s2.ap
        kw = {}
        writes = [out.buf]
        if op1 is not None:
            kw["op1"] = op1
        if accum is not None:
            kw["accum_out"] = accum.ap
            writes.append(accum.buf)
        return self.op(E, lambda: E.eng.tensor_scalar(out=out.ap, in0=in0.ap, scalar1=a1, scalar2=a2, op0=op0, **kw),
                       reads, writes)

    def stt(self, out, in0, scalar, in1, op0, op1):
        reads = [in0.buf, in1.buf]
        sc = scalar
        if isinstance(scalar, V):
            reads.append(scalar.buf)
            sc = scalar.ap
        return self.op(self.dve, lambda: self.nc.vector.scalar_tensor_tensor(out=out.ap, in0=in0.ap, scalar=sc,
                                                                              in1=in1.ap, op0=op0, op1=op1),
                       reads, [out.buf])

    def cp(self, E, out, in_):
        if E is self.act:
            return self.actf(out, in_, AF.Copy)
        return self.op(E, lambda: E.eng.tensor_copy(out=out.ap, in_=in_.ap), [in_.buf], [out.buf])

    def red(self, out, in_, op=ALU.add):
        return self.op(self.dve, lambda: self.nc.vector.tensor_reduce(out=out.ap, in_=in_.ap, axis=AX.X, op=op),
                       [in_.buf], [out.buf])

    def recip(self, out, in_):
        return self.op(self.dve, lambda: self.nc.vector.reciprocal(out=out.ap, in_=in_.ap), [in_.buf], [out.buf])

    def rsqrt(self, out, in_, scale, post=None):
        self.ts(self.dve, out, in_, scale, EPS, ALU.mult, ALU.add)
        self.actf(out, out, AF.Sqrt)
        self.recip(out, out)
        if post is not None:
            self.ts(self.dve, out, out, post, None, ALU.mult)

    def memset(self, E, out, val):
        return self.op(E, lambda: E.eng.memset(out.ap, val), [], [out.buf])


def row_layout(c):
    items = [("cw", 4 * c.CONVD), ("cb", c.CONVD), ("dtb", c.SH), ("alog", c.SH), ("sd", c.SH),
             ("sng", c.SW), ("qg", 64), ("kg", 64), ("lam", 256), ("og", 128), ("ba", c.GH * 64), ("gng", 128)]
    lay = {}
    o = 0
    for k, n in items:
        lay[k] = (o, n)
        o += n
    return lay, o


def wlin_layout(c):
    n_own = c.NKT
    n_rem = (c.NCORE - 1) * (2 * c.NKT - 1)
    n_cache = c.NPT
    return n_own, n_rem, n_cache, n_own + n_rem + n_cache


CONST_COLS = 128 * 4


def build_program(c, debug=False, stop_after=None, part=0, layer_idx=None):
    nc = bass.Bass("TRN2", target_bir_lowering=False)
    K = Builder(nc)
    D, KC, INC, NT, NTOK = c.D, c.KC, c.INC, c.NT, c.NTOK
    DH, GH, SH, HG, SW, CONVD = c.DH, c.GH, c.SH, c.HG, c.SW, c.CONVD
    TOKP, NS, NCORE = c.TOKP, c.NS, c.NCORE
    HW = HG * 64
    lay, RW = row_layout(c)
    n_own, n_rem, n_cache, NW = wlin_layout(c)
    L = c.DEPTH

    NEED_IN = {
        1: {"xin", "g1T", "w_in", "rowp", "consts"},
        2: {"rowp", "consts", "sel", "wa2"},
        3: {"xin", "ck", "cv", "sconv", "sssd", "sgla", "g2T", "w_out", "w1", "w2", "rowp", "wa2", "consts",
            "gate", "sel", "wlin", "bdiag"},
    }
    NEED_OUT = {
        1: {"kout", "vout", "convp", "convs"},
        2: set(),
        3: {"yout", "ssdp", "ssds", "glap", "glas"},
    }
    CROSS = {"Pscr": (1, {2, 3}), "QTd": (1, {3}), "KTin": (1, {3}), "VPin": (1, {3}), "KTS": (1, {3}),
             "VPS": (1, {3}), "HALOin": (1, set()), "STin": (2, set()), "GSin": (2, set()),
             "KTall": (0, {3}), "VPall": (0, {3}), "HALOall": (0, {2, 3}), "STall": (0, {3}), "GSall": (0, {3})}

    def din(name, shape, dt=F32):
        if part and name not in NEED_IN[part]:
            return K.dram(name, shape, dt)
        return K.dram(name, shape, dt, kind="ExternalInput")

    def dout(name, shape, dt=F32):
        if part and name not in NEED_OUT[part]:
            return K.dram(name, shape, dt)
        return K.dram(name, shape, dt, kind="ExternalOutput")

    def dscr(name, shape, dt, kind=None):
        if part and name in CROSS:
            prod, cons = CROSS[name]
            if prod == part:
                return K.dram(name, shape, dt, kind="ExternalOutput")
            if part in cons:
                return K.dram(name, shape, dt, kind="ExternalInput")
        return K.dram(name, shape, dt, kind=kind)

    xin = din("xin", [NTOK, D])
    ck = din("ck", [L, NS, c.PAST, DH * 128])
    cv = din("cv", [L, NS, c.PAST, DH * 128])
    sconv = din("sconv", [L, NS, 3, CONVD])
    sssd = din("sssd", [L, NS, SH * 64, 128])
    sgla = din("sgla", [L, NS, GH * 64, 128])
    g1T = din("g1T", [L, 128, KC])
    g2T = din("g2T", [L, 128, KC])
    w_in = din("w_in", [L, D, INC])
    w_out = din("w_out", [L, D, D])
    w1 = din("w1", [L, D, c.DFF])
    w2 = din("w2", [L, c.DFF, D])
    rowp = din("rowp", [L, RW])
    wa2 = din("wa2", [L, 16, GH * 64])
    consts = din("consts", [128, CONST_COLS])
    gate_d = din("gate", [128, 8])
    sel_d = din("sel", [128, 8])
    wlin_d = din("wlin", [128, NW * DH])
    bdiag_d = din("bdiag", [128, DH * 128])

    yout = dout("yout", [NTOK, D])
    kout = dout("kout", [L, NTOK, DH * 128])
    vout = dout("vout", [L, NTOK, DH * 128])
    convp = dout("convp", [L, 3, CONVD])
    convs = dout("convs", [L, NS, 3, CONVD])
    ssdp = dout("ssdp", [L, SH * 64, 128])
    ssds = dout("ssds", [L, NS, SH * 64, 128])
    glap = dout("glap", [L, GH * 64, 128])
    glas = dout("glas", [L, NS, GH * 64, 128])

    kind_dbg = "ExternalOutput" if debug else None
    P = dscr("Pscr", [NTOK, INC], F32, kind=kind_dbg)
    X1 = K.dram("X1scr", [NTOK, D], F32)
    XM = K.dram("XMscr", [NTOK, D], F32, kind=kind_dbg)
    QTd = dscr("QTd", [DH * 128, NTOK], BF16)
    KTin = dscr("KTin", [DH * 128, TOKP], BF16)
    KTall = dscr("KTall", [NCORE * DH * 128, TOKP], BF16)
    VPin = dscr("VPin", [TOKP, DH * 132], BF16)
    VPall = dscr("VPall", [NCORE * TOKP, DH * 132], BF16)
    KTS = dscr("KTS", [DH * 128, NS * 64], BF16)
    VPS = dscr("VPS", [NS * 64, DH * 132], BF16)
    HALOin = dscr("HALOin", [3, CONVD], F32)
    HALOall = dscr("HALOall", [NCORE * 3, CONVD], F32)
    HALOsel = K.dram("HALOsel", [3, CONVD], F32)
    SWD = SH * 64 + SH
    STin = dscr("STin", [128, SWD], F32)
    STall = dscr("STall", [NCORE * 128, SWD], F32)
    GP = GH // 2
    GWD = GH * 128 + GH
    GSin = dscr("GSin", [64, GWD], F32)
    GSall = dscr("GSall", [NCORE * 64, GWD], F32)
    MIXT = K.dram("MIXT", [D, NTOK], BF16)
    MIXTv = MIXT.rr("(k p) t -> p k t", p=128)

    pe, dve, act, pool, sp = K.pe, K.dve, K.act, K.pool, K.sp

    cst = K.tile("cst", [128, CONST_COLS], F32)
    cstb = K.tile("cstb", [128, CONST_COLS], BF16)
    K.dma(sp, cst, consts)
    K.cp(dve, cstb, cst)
    identf, trif, mstrf, onesf = (cst[:, i * 128:(i + 1) * 128] for i in range(4))
    identb, trib, mstrb, onesb = (cstb[:, i * 128:(i + 1) * 128] for i in range(4))
    gate = K.tile("gate_sb", [128, 8], F32)
    sel = K.tile("sel_sb", [128, 8], F32)
    K.dma(sp, gate, gate_d)
    K.dma(sp, sel, sel_d)
    wtab = K.tile("wtab", [128, NW * DH], F32)
    K.dma(sp, wtab, wlin_d)
    K.actf(wtab, wtab, AF.Exp)
    bdiag = K.tile("bdiag_sb", [128, DH * 128], F32)
    K.dma(sp, bdiag, bdiag_d)
    gT = K.tile("gT", [128, KC], F32)
    psum = Pool(K, "ps", [128, 512], F32, 8, space="psum")
    mixstage = Pool(K, "mixst", [128, 4, 128], BF16, 4)

    def mix_store(kc0, nk_, lt0, n, psview):
        stg = mixstage.next()
        K.cp(act, stg[:, 0:nk_, 0:n], psview)
        K.dma(sp, MIXTv[:, kc0:kc0 + nk_, lt0:lt0 + n], stg[:, 0:nk_, 0:n])

    def R(key, n=None, parts=128):
        o, ln = lay[key]
        return cur["rows"][0:parts, o:o + (ln if n is None else n)]

    def norm_to_hT(src, gsrc, xpool, xnpool, stpool, tt, dst=None):
        xs = xpool.next()
        K.dma(sp, xs, src[tt * 128:(tt + 1) * 128, :])
        return xs

    def rms_tile_to_hT(hT, xs, xnpool, stpool, tt):
        st = stpool.next()
        xn = xnpool.next()
        K.actf(xn, xs, AF.Square, accum=st[:, 0:1])
        K.rsqrt(st[:, 2:3], st[:, 0:1], 1.0 / D)
        K.actf(xn, xs, AF.Copy, scale=st[:, 2:3])
        for kg in range(KC // 4):
            ps = psum.next()
            psb = ps.bitcast(BF16)
            for j in range(4):
                K.tr(psb[:, j * 128:(j + 1) * 128], xn[:, (kg * 4 + j) * 128:(kg * 4 + j + 1) * 128], identb)
            K.tt(dve, hT[:, kg * 4:(kg + 1) * 4, tt * 128:(tt + 1) * 128],
                 psb[:, 0:512].rr("p (a b) -> p a b", a=4),
                 gT[:, kg * 4:(kg + 1) * 4].unsq(2).bc([128, 4, 128]), ALU.mult)

    def scope():
        return ExitStack()

    class Scope:
        def __enter__(self):
            self.es = ExitStack()
            self.saved = K.stack
            K.stack = self.es
            K.scope_bufs.append([])
            return self

        def __exit__(self, *a):
            K.stack = self.saved
            self.es.close()
            f = dict(K.fence)
            for b in K.scope_bufs.pop():
                for d in (b.r, b.w):
                    for sid, v in d.items():
                        if f.get(sid, 0) < v:
                            f[sid] = v
            K.fence = f
            return False

    def phase_A(l):
        xsrc = xin if (l == 0 or part) else X1
        K.dma(sp, gT, g1T[l])
        with Scope():
            hT = K.tile(f"hT{l}", [128, KC, NTOK], BF16)
            with Scope():
                xpool = Pool(K, f"xa{l}_", [128, D], F32, 2)
                xnpool = Pool(K, f"xn{l}_", [128, D], BF16, 2)
                stpool = Pool(K, f"st{l}_", [128, 4], F32, 3)
                nxt = xpool.next()
                K.dma(sp, nxt, xsrc[0:128, :])
                for tt in range(NT):
                    xs = nxt
                    if tt + 1 < NT:
                        nxt = xpool.next()
                        K.dma(sp, nxt, xsrc[(tt + 1) * 128:(tt + 2) * 128, :])
                    rms_tile_to_hT(hT, xs, xnpool, stpool, tt)
            with Scope():
                wpool = Pool(K, f"wi{l}_", [128, KC, 512], BF16, 2)
                stage = Pool(K, f"sg{l}_", [128, 512], F32, 3)
                ncb = (INC + 511) // 512
                wv = w_in[l].rr("(k p) c -> p k c", p=128)

                def loadw(cb):
                    c0 = cb * 512
                    cw_ = min(512, INC - c0)
                    wb = wpool.next()
                    K.dma(pool, wb[:, :, 0:cw_], wv[:, :, c0:c0 + cw_])
                    return wb
                nxtw = loadw(0)
                for cb in range(ncb):
                    wb = nxtw
                    if cb + 1 < ncb:
                        nxtw = loadw(cb + 1)
                    c0 = cb * 512
                    cw_ = min(512, INC - c0)
                    for tt in range(NT):
                        ps = psum.next()
                        for kc in range(KC):
                            K.mm(ps[:, 0:cw_], hT[:, kc, tt * 128:(tt + 1) * 128], wb[:, kc, 0:cw_],
                                 start=(kc == 0), stop=(kc == KC - 1))
                        sg = stage.next()
                        K.cp(act if (tt % 2 == 0) else dve, sg[:, 0:cw_], ps[:, 0:cw_])
                        K.dma(sp, P[tt * 128:(tt + 1) * 128, c0:c0 + cw_], sg[:, 0:cw_])

    def phase_A3(l):
        G = 4 * DH
        with Scope():
            qkpool = Pool(K, f"qk{l}_", [128, 2 * DH * 128], F32, 2)
            sqpool = Pool(K, f"sq{l}_", [128, 2 * DH * 128], F32, 1)
            vpool = Pool(K, f"vv{l}_", [128, DH * 128], F32, 2)
            stat = Pool(K, f"sa{l}_", [128, G], F32, 2)
            bpool = Pool(K, f"qb{l}_", [128, 2 * DH * 128], BF16, 2)
            tpool = Pool(K, f"tq{l}_", [128, DH, 128], BF16, 3)
            vppool = Pool(K, f"vp{l}_", [128, DH, 132], BF16, 2)
            for vp in vppool.items:
                K.memset(dve, vp[:, :, 128:129], 1.0)
                K.memset(dve, vp[:, :, 129:132], 0.0)
            QTv = QTd.rr("(h p) t -> p h t", p=128)
            KTinv = KTin.rr("(h p) t -> p h t", p=128)
            KTSv = KTS.rr("(h p) t -> p h t", p=128)
            for tt in range(NT):
                r0, r1 = tt * 128, (tt + 1) * 128
                qk = qkpool.next()
                K.dma(sp, qk, P[r0:r1, c.oDQ:c.oDQ + 2 * DH * 128])
                vv = vpool.next()
                K.dma(sp, vv, P[r0:r1, c.oDV:c.oDV + DH * 128])
                K.dma(sp, vout[l, r0:r1, :], P[r0:r1, c.oDV:c.oDV + DH * 128])
                sq = sqpool.next()
                K.actf(sq, qk, AF.Square)
                ss = stat.next()
                K.red(ss, sq.rr("p (g d) -> p g d", d=64))
                K.rsqrt(ss, ss, 1.0 / 64)
                qk3 = qk.rr("p (g d) -> p g d", d=64)
                K.tt(dve, qk3, qk3, ss.unsq(2).bc([128, G, 64]), ALU.mult)
                K.tt(dve, qk3[:, 0:2 * DH, :], qk3[:, 0:2 * DH, :], R("qg").unsq(1).bc([128, 2 * DH, 64]), ALU.mult)
                K.tt(dve, qk3[:, 2 * DH:G, :], qk3[:, 2 * DH:G, :], R("kg").unsq(1).bc([128, 2 * DH, 64]), ALU.mult)
                K.dma(sp, kout[l, r0:r1, :], qk[:, DH * 128:2 * DH * 128])
                qkb = bpool.next()
                K.cp(act, qkb, qk)
                for half in range(2):
                    ps = psum.next()
                    psb = ps.bitcast(BF16)
                    for h in range(DH):
                        K.tr(psb[:, h * 128:(h + 1) * 128], qkb[:, (half * DH + h) * 128:(half * DH + h + 1) * 128], identb)
                    tq = tpool.next()
                    K.cp(dve, tq, psb[:, 0:DH * 128].rr("p (h t) -> p h t", h=DH))
                    if half == 0:
                        dest = QTv[:, :, r0:r1]
                    elif tt < c.NTP:
                        dest = KTinv[:, :, r0:r1]
                    else:
                        dest = KTSv[:, :, r0 - TOKP:r1 - TOKP]
                    K.dma(sp, dest, tq)
                vp = vppool.next()
                K.cp(act, vp[:, :, 0:128], vv.rr("p (h v) -> p h v", h=DH))
                if tt < c.NTP:
                    dest = VPin[r0:r1, :].rr("t (h v) -> t h v", h=DH)
                else:
                    dest = VPS[r0 - TOKP:r1 - TOKP, :].rr("t (h v) -> t h v", h=DH)
                K.dma(sp, dest, vp)
            xc0, xc1 = c.oXBC, c.oXBC + CONVD
            K.dma(sp, HALOin, P[TOKP - 3:TOKP, xc0:xc1])
            K.dma(sp, convp[l], P[TOKP - 3:TOKP, xc0:xc1])
            for i in range(NS):
                K.dma(sp, convs[l, i], P[TOKP + 64 * i + 61:TOKP + 64 * i + 64, xc0:xc1])

    def phase_gather1(l):
        if not part:
            K.allgather(KTall, KTin, NCORE)
            K.allgather(VPall, VPin, NCORE)
            K.allgather(HALOall, HALOin, NCORE)
        with Scope():
            hal = K.tile(f"hal{l}", [3, NCORE, CONVD], F32)
            acc = K.tile(f"hacc{l}", [3, CONVD], F32)
            K.dma(sp, hal, HALOall.rr("(r i) c -> i r c", i=3))
            K.ts(dve, acc, hal[:, 0, :], sel[0:3, 0:1], None, ALU.mult)
            for r in range(1, NCORE):
                K.stt(acc, hal[:, r, :], sel[0:3, r:r + 1], acc, ALU.mult, ALU.add)
            K.dma(sp, HALOsel, acc)

    GK = GH * 64
    KC_S = SW // 128
    KC_A = KC_S + DH
    xc0, xc1 = c.oXBC, c.oXBC + CONVD

    def phase_scan(l, mixT, mode=None):
        with Scope():
            xcin = Pool(K, f"xci{l}_", [64, 4, CONVD], F32, 1)
            accp = Pool(K, f"cac{l}_", [64, CONVD], F32, 1)
            tmpc = Pool(K, f"ctm{l}_", [64, CONVD], F32, 2)
            xcp = Pool(K, f"xc{l}_", [64, CONVD], F32, 2)
            xcbp = Pool(K, f"xcb{l}_", [64, 512], BF16, 2)
            sm64 = Pool(K, f"s64{l}_", [64, 16 + SH], F32, 12)
            sm128 = Pool(K, f"s128{l}_", [128, SH], F32, 3)
            xdp = Pool(K, f"xd{l}_", [64, SH, 64], BF16, 3)
            bctp = Pool(K, f"bct{l}_", [128, 4, 64], BF16, 2)
            m2p = Pool(K, f"m2{l}_", [64, HG, 64], F32, 2)
            lgp = Pool(K, f"lg{l}_", [64, HG, 64], F32, 2)
            ap_ = Pool(K, f"aa{l}_", [64, HG, 64], BF16, 2)
            gmp = Pool(K, f"gm{l}_", [64, 64], F32, 2)
            t1p = Pool(K, f"t1{l}_", [64, HW], F32, 2)
            t2p = Pool(K, f"t2{l}_", [64, HW], F32, 2)
            zp = Pool(K, f"z{l}_", [64, HW], F32, 2)
            ybp = Pool(K, f"yb{l}_", [64, HW], BF16, 2)
            jkp = Pool(K, f"jk{l}_", [64, HW], BF16, 1)
            ginp = Pool(K, f"gin{l}_", [64, 2 * GK + c.GW], F32, 2)
            gap = Pool(K, f"ga{l}_", [64, 16], F32, 2)
            gatp = Pool(K, f"gat{l}_", [16, 64], F32, 2)
            lap = Pool(K, f"la{l}_", [64, GK], F32, 4)
            kbp = Pool(K, f"kb{l}_", [64, GK], BF16, 2)
            vbp = Pool(K, f"vb{l}_", [64, c.GW], BF16, 2)
            ebp = Pool(K, f"eb{l}_", [64, GH, 64], F32, 4)
            qkt = Pool(K, f"qkt{l}_", [64, GH, 64], BF16, 4)
            attp = Pool(K, f"att{l}_", [64, GH, 64], BF16, 2)
            ogp = Pool(K, f"og{l}_", [64, c.GW], F32, 3)
            ogbp = Pool(K, f"ogb{l}_", [64, c.GW], BF16, 2)
            dgp = Pool(K, f"dg{l}_", [64, GH], F32, 2)
            stfp = Pool(K, f"stf{l}_", [128, 4, 128], F32, 2)
            blkp = Pool(K, f"blk{l}_", [128, SWD], F32, 2)
            HT = K.tile(f"HT{l}", [128, SH * 64], F32)
            HTb = K.tile(f"HTb{l}", [128, SH * 64], BF16)
            LOG = K.tile(f"LOG{l}", [128, SH], F32)
            Htmp = K.tile(f"Htmp{l}", [128, SH * 64], F32)
            S = K.tile(f"S{l}", [64, GH, 128], F32)
            Sb = K.tile(f"Sb{l}", [64, GH, 128], BF16)
            LOGG = K.tile(f"LOGG{l}", [64, GH], F32)
            Stmp = K.tile(f"Stmp{l}", [64, GH, 128], F32)
            wa2sb = K.tile(f"wa2sb{l}", [16, GK], F32)
            K.dma(sp, wa2sb, wa2[l])
            Arow = K.tile(f"Arow{l}", [64, SH], F32)
            K.actf(Arow, R("alog", parts=64), AF.Exp)
            K.ts(dve, Arow, Arow, -1.0, None, ALU.mult)
            cw = R("cw", parts=64).rr("p (j c) -> p j c", j=4)

            FLV = int(os.environ.get("SSD_LEVEL", "9"))

            def ssd_chunk(lt0, prev, full, use_log):
                rows_ = slice(lt0, lt0 + 64)
                xci = xcin.next()
                if prev is None:
                    for j in range(4):
                        K.dma(sp, xci[:, j, :], P[lt0 - 3 + j:lt0 - 3 + j + 64, xc0:xc1])
                else:
                    K.dma(sp, xci[:, 3, :], P[rows_, xc0:xc1])
                    for j in range(3):
                        K.dma(sp, xci[3 - j:64, j, :], P[lt0:lt0 + 64 - (3 - j), xc0:xc1])
                        K.dma(sp, xci[0:3 - j, j, :], prev[j:3, :])
                acc = accp.next()
                K.tt(dve, acc, xci[:, 0, :], cw[:, 0, :], ALU.mult)
                for j in range(1, 4):
                    tm = tmpc.next()
                    K.tt(dve, tm, xci[:, j, :], cw[:, j, :], ALU.mult)
                    K.tt(dve, acc, acc, tm, ALU.add)
                K.tt(dve, acc, acc, R("cb", parts=64), ALU.add)
                xc = xcp.next()
                K.actf(xc, acc, AF.Silu)
                xcb = xcbp.next()
                K.cp(dve, xcb, xc[:, SW:SW + 512])
                dtr = sm64.next()[:, 0:SH]
                K.dma(sp, dtr, P[rows_, c.oDT:c.oDT + SH])
                K.tt(dve, dtr, dtr, R("dtb", parts=64), ALU.add)
                K.actf(dtr, dtr, AF.Exp)
                dt = sm64.next()[:, 0:SH]
                K.actf(dt, dtr, AF.Ln, bias=1.0)
                dta = sm64.next()[:, 0:SH]
                K.tt(dve, dta, dt, Arow, ALU.mult)
                psA = psum.next()
                K.mm(psA[0:64, 0:SH], trif[0:64, 0:64], dta)
                K.mm(psA[0:128, 64:64 + SH], onesf[0:64, 0:128], dta)
                cs = sm64.next()[:, 0:SH]
                K.cp(dve, cs, psA[0:64, 0:SH])
                dd = sm64.next()[:, 0:SH]
                K.tt(dve, dd, psA[0:64, 64:64 + SH], cs, ALU.subtract)
                K.actf(dd, dd, AF.Exp)
                K.tt(dve, dd, dd, dt, ALU.mult)
                decB = sm128.next()
                K.actf(decB, psA[:, 64:64 + SH], AF.Exp)
                if use_log:
                    K.tt(dve, LOG, LOG, psA[:, 64:64 + SH], ALU.add)
                xs3 = xc[:, 0:SW].rr("p (h d) -> p h d", d=64)
                xde = xdp.next()
                K.tt(dve, xde, xs3, dd.unsq(2).bc([64, SH, 64]), ALU.mult)
                psS = []
                for g in range(2):
                    ps_ = psum.next()
                    K.mm(ps_[:, 0:HW], xcb[0:64, g * 128:(g + 1) * 128],
                         xde[:, g * HG:(g + 1) * HG, :].rr("p h d -> p (h d)"))
                    psS.append(ps_)
                psO = []
                if full:
                    ecs = sm64.next()[:, 0:SH]
                    K.actf(ecs, cs, AF.Exp)
                    psT = psum.next()
                    psTb = psT.bitcast(BF16)
                    for j in range(4):
                        K.tr(psTb[:, j * 64:(j + 1) * 64], xcb[0:64, j * 128:(j + 1) * 128], identb[0:64, 0:64])
                    bct = bctp.next()
                    K.cp(dve, bct, psTb[:, 0:256].rr("p (j t) -> p j t", j=4))
                    for g in range(2):
                        ps_ = psum.next()
                        K.mm(ps_[0:64, 0:HW], bct[:, 2 + g, :], HTb[:, g * HW:(g + 1) * HW])
                        psO.append(ps_)
                for g in range(2):
                    HTg = HT[:, g * HW:(g + 1) * HW]
                    HTg3 = HTg.rr("p (h d) -> p h d", d=64)
                    K.tt(dve, HTg3, HTg3, decB[:, g * HG:(g + 1) * HG].unsq(2).bc([128, HG, 64]), ALU.mult)
                    K.tt(dve, HTg, HTg, psS[g][:, 0:HW], ALU.add)
                if not full or FLV < 2:
                    return
                xdt = xdp.next()
                K.tt(dve, xdt, xs3, dt.unsq(2).bc([64, SH, 64]), ALU.mult)
                for g in range(2):
                    hs = slice(g * HG, (g + 1) * HG)
                    psG = psum.next()
                    K.mm(psG[0:64, 0:64], bct[:, g, :], bct[:, 2 + g, :])
                    Gm = gmp.next()
                    K.tt(dve, Gm, psG[0:64, 0:64], trif[0:64, 0:64], ALU.mult)
                    M2 = m2p.next()
                    K.tt(dve, M2, trif[0:64, 0:64].unsq(1).bc([64, HG, 64]),
                         dta[:, hs].unsq(2).bc([64, HG, 64]), ALU.mult)
                    psL = psum.next()
                    K.mm(psL[0:64, 0:HW], mstrf[0:64, 0:64], M2.rr("p h t -> p (h t)"))
                    Lg = lgp.next()
                    K.actf(Lg.rr("p h t -> p (h t)"), psL[0:64, 0:HW], AF.Exp)
                    A = ap_.next()
                    K.tt(dve, A, Lg, Gm.unsq(1).bc([64, HG, 64]), ALU.mult)
                    psY = psum.next()
                    for hh in range(HG):
                        K.mm(psY[0:64, hh * 64:(hh + 1) * 64], A[:, hh, :], xdt[:, g * HG + hh, :])
                    if FLV < 3:
                        continue
                    t1 = t1p.next()
                    t13 = t1.rr("p (h d) -> p h d", d=64)
                    K.tt(dve, t13, psO[g][0:64, 0:HW].rr("p (h d) -> p h d", d=64),
                         ecs[:, hs].unsq(2).bc([64, HG, 64]), ALU.mult)
                    K.tt(dve, t1, t1, psY[0:64, 0:HW], ALU.add)
                    t2 = t2p.next()
                    K.tt(dve, t2.rr("p (h d) -> p h d", d=64), xs3[:, hs, :],
                         R("sd", parts=64)[:, hs].unsq(2).bc([64, HG, 64]), ALU.mult)
                    K.tt(dve, t1, t1, t2, ALU.add)
                    z = zp.next()
                    K.dma(sp, z, P[rows_, c.oZ + g * HW:c.oZ + (g + 1) * HW])
                    K.actf(z, z, AF.Silu)
                    K.tt(dve, t1, t1, z, ALU.mult)
                    st = sm64.next()
                    jk = jkp.next()
                    K.actf(jk, t1, AF.Square, accum=st[:, 0:1])
                    K.rsqrt(st[:, 2:3], st[:, 0:1], 1.0 / HW)
                    yb = ybp.next()
                    K.stt(yb, t1, st[:, 2:3], R("sng", parts=64)[:, g * HW:(g + 1) * HW], ALU.mult, ALU.mult)
                    if FLV < 4:
                        continue
                    psT2 = psum.next()
                    psT2b = psT2.bitcast(BF16)
                    nj = HW // 128
                    for j in range(nj):
                        K.tr(psT2b[:, j * 64:(j + 1) * 64], yb[0:64, j * 128:(j + 1) * 128], identb[0:64, 0:64])
                    mix_store(g * nj, nj, lt0, 64, psT2b[:, 0:nj * 64].rr("p (j t) -> p j t", t=64))
                K.cp(act, HTb, HT)

            def gla_chunk(lt0, full, use_log):
                rows_ = slice(lt0, lt0 + 64)
                gin = ginp.next()
                K.dma(sp, gin, P[rows_, c.oGQ:c.oGQ + 2 * GK + c.GW])
                ga = gap.next()
                K.dma(sp, ga, P[rows_, c.oGA:c.oGA + 16])
                psa = psum.next()
                K.tr(psa[0:16, 0:64], ga, identf[0:64, 0:64])
                gaT = gatp.next()
                K.cp(dve, gaT, psa[0:16, 0:64])
                K.mm(psa[0:64, 128:128 + GK], gaT, wa2sb)
                la = lap.next()
                K.tt(dve, la, psa[0:64, 128:128 + GK], R("ba", parts=64), ALU.add)
                K.actf(la, la, AF.Exp, scale=-1.0)
                K.actf(la, la, AF.Ln, bias=1.0)
                K.ts(dve, la, la, -1.0 / 16.0, None, ALU.mult)
                psb1 = psum.next()
                K.mm(psb1[0:64, 0:GK], trif[0:64, 0:64], la)
                K.mm(psb1[0:64, GK:2 * GK], onesf[0:64, 0:64], la)
                psb2 = psum.next()
                for h in range(GH):
                    K.mm(psb2[0:64, h * 64:(h + 1) * 64], la[:, h * 64:(h + 1) * 64], trif[0:64, 0:64])
                bsb = lap.next()
                K.cp(dve, bsb, psb1[0:64, 0:GK])
                eb = lap.next()
                K.tt(dve, eb, psb1[0:64, GK:2 * GK], bsb, ALU.subtract)
                K.actf(eb, eb, AF.Exp)
                kend = kbp.next()
                K.tt(dve, kend, gin[:, GK:2 * GK], eb, ALU.mult)
                vb = vbp.next()
                K.cp(act, vb, gin[:, 2 * GK:2 * GK + c.GW])
                psSg = psum.next()
                for h in range(GH):
                    K.mm(psSg[0:64, h * 128:(h + 1) * 128], kend[:, h * 64:(h + 1) * 64], vb[:, h * 128:(h + 1) * 128])
                decg = dgp.next()
                for h in range(GH):
                    K.actf(decg[:, h:h + 1], psb2[0:64, h * 64 + 63:h * 64 + 64], AF.Exp)
                    if use_log:
                        K.tt(dve, LOGG[:, h:h + 1], LOGG[:, h:h + 1], psb2[0:64, h * 64 + 63:h * 64 + 64], ALU.add)
                if full:
                    ebT = ebp.next()
                    enbT = ebp.next()
                    K.actf(ebT.rr("p g t -> p (g t)"), psb2[0:64, 0:GH * 64], AF.Exp)
                    K.actf(enbT.rr("p g t -> p (g t)"), psb2[0:64, 0:GH * 64], AF.Exp, scale=-1.0)
                    psq = psum.next()
                    for h in range(GH):
                        K.tr(psq[0:64, h * 64:(h + 1) * 64], gin[:, h * 64:(h + 1) * 64], identf[0:64, 0:64])
                        K.tr(psq[0:64, (GH + h) * 64:(GH + h + 1) * 64], gin[:, GK + h * 64:GK + (h + 1) * 64],
                             identf[0:64, 0:64])
                    qtT = qkt.next()
                    ktT = qkt.next()
                    K.stt(qtT.rr("p g t -> p (g t)"), psq[0:64, 0:GH * 64], 0.125, ebT.rr("p g t -> p (g t)"),
                          ALU.mult, ALU.mult)
                    K.tt(dve, ktT.rr("p g t -> p (g t)"), psq[0:64, GH * 64:2 * GH * 64], enbT.rr("p g t -> p (g t)"),
                         ALU.mult)
                    psAt = psum.next()
                    for h in range(GH):
                        K.mm(psAt[0:64, h * 64:(h + 1) * 64], ktT[:, h, :], qtT[:, h, :])
                    attm = attp.next()
                    K.tt(dve, attm, psAt[0:64, 0:GH * 64].rr("p (h t) -> p h t", t=64),
                         trif[0:64, 0:64].unsq(1).bc([64, GH, 64]), ALU.mult)
                    psOg = psum.next()
                    for h in range(GH):
                        K.mm(psOg[0:64, h * 128:(h + 1) * 128], attm[:, h, :], vb[:, h * 128:(h + 1) * 128],
                             start=True, stop=False)
                        K.mm(psOg[0:64, h * 128:(h + 1) * 128], qtT[:, h, :], Sb[:, h, :], start=False, stop=True)
                for h in range(GH):
                    K.stt(S[:, h, :], S[:, h, :], decg[:, h:h + 1], psSg[0:64, h * 128:(h + 1) * 128],
                          ALU.mult, ALU.add)
                if not full:
                    return
                K.cp(act, Sb, S)
                og = ogp.next()
                sq = ogp.next()
                K.cp(dve, og, psOg[0:64, 0:c.GW])
                K.actf(sq, og, AF.Square)
                st = sm64.next()
                K.red(st[:, 0:GH], sq.rr("p (h v) -> p h v", v=128))
                K.rsqrt(st[:, 0:GH], st[:, 0:GH], 1.0 / 128)
                og3 = og.rr("p (h v) -> p h v", v=128)
                K.tt(dve, og3, og3, st[:, 0:GH].unsq(2).bc([64, GH, 128]), ALU.mult)
                K.tt(dve, og3, og3, R("gng", parts=64).unsq(1).bc([64, GH, 128]), ALU.mult)
                gg = ogp.next()
                K.dma(sp, gg, P[rows_, c.oGG:c.oGG + c.GW])
                K.actf(gg, gg, AF.Silu)
                ogb = ogbp.next()
                K.tt(dve, ogb, og, gg, ALU.mult)
                psT2 = psum.next()
                psT2b = psT2.bitcast(BF16)
                nj = c.GW // 128
                for j in range(nj):
                    K.tr(psT2b[:, j * 64:(j + 1) * 64], ogb[0:64, j * 128:(j + 1) * 128], identb[0:64, 0:64])
                mix_store(KC_A, nj, lt0, 64, psT2b[:, 0:nj * 64].rr("p (j t) -> p j t", t=64))

            def write_ssd_state(dst):
                nblk = SH * 64 // 128
                for b0 in range(0, nblk, 4):
                    ps_ = psum.next()
                    nb = min(4, nblk - b0)
                    for j in range(nb):
                        K.tr(ps_[:, j * 128:(j + 1) * 128], HT[:, (b0 + j) * 128:(b0 + j + 1) * 128], identf)
                    sf = stfp.next()
                    K.cp(dve, sf[:, 0:nb, :], ps_[:, 0:nb * 128].rr("p (j n) -> p j n", n=128))
                    K.dma(sp, dst[b0 * 128:(b0 + nb) * 128, :].rr("(j p) n -> p j n", p=128), sf[:, 0:nb, :])

            def load_ssd_state(src):
                nblk = SH * 64 // 128
                for b0 in range(0, nblk, 4):
                    nb = min(4, nblk - b0)
                    sf = stfp.next()
                    K.dma(sp, sf[:, 0:nb, :], src[b0 * 128:(b0 + nb) * 128, :].rr("(j p) n -> p j n", p=128))
                    ps_ = psum.next()
                    for j in range(nb):
                        K.tr(ps_[:, j * 128:(j + 1) * 128], sf[:, j, :], identf)
                    K.cp(dve, HT[:, b0 * 128:(b0 + nb) * 128], ps_[:, 0:nb * 128])
                K.cp(act, HTb, HT)

            SLV = int(os.environ.get("SCAN_LEVEL", "9"))
            SPR = 16
            if mode != "p2":
                K.memset(dve, HT, 0.0)
                K.memset(dve, LOG, 0.0)
                K.memset(dve, S, 0.0)
                K.memset(dve, LOGG, 0.0)
                for ch in range(c.NCHP):
                    ssd_chunk(ch * 64, HALOsel if ch == 0 else None, False, True)
                    if SLV >= 2:
                        gla_chunk(ch * 64, False, True)
                if SLV < 3:
                    return
                K.dma(sp, STin[:, 0:SH * 64], HT)
                K.dma(sp, STin[:, SH * 64:SWD], LOG, slow=True)
                K.dma(sp, GSin[:, 0:GH * 128], S.rr("p g v -> p (g v)"))
                K.dma(sp, GSin[:, GH * 128:GWD], LOGG, slow=True)
                if mode == "p1":
                    return
                for q in range(128 // SPR):
                    K.allgather(STall[q * NCORE * SPR:(q + 1) * NCORE * SPR, :], STin[q * SPR:(q + 1) * SPR, :], NCORE)
                K.allgather(GSall, GSin, NCORE)
            K.memset(dve, HT, 0.0)
            K.memset(dve, S, 0.0)
            for r in range(NCORE - 1):
                blk = blkp.next()
                for q in range(128 // SPR):
                    K.dma(sp, blk[q * SPR:(q + 1) * SPR, :],
                          STall[q * NCORE * SPR + r * SPR:q * NCORE * SPR + (r + 1) * SPR, :])
                dec = sm128.next()
                K.actf(dec, blk[:, SH * 64:SWD], AF.Exp)
                K.tt(dve, Htmp.rr("p (h d) -> p h d", d=64), HT.rr("p (h d) -> p h d", d=64),
                     dec.unsq(2).bc([128, SH, 64]), ALU.mult)
                K.tt(dve, Htmp, Htmp, blk[:, 0:SH * 64], ALU.add)
                K.tt(dve, Htmp, Htmp, HT, ALU.subtract)
                K.stt(HT, Htmp, gate[:, r:r + 1], HT, ALU.mult, ALU.add)
                gb = blkp.next()
                K.dma(sp, gb[0:64, 0:GWD], GSall[r * 64:(r + 1) * 64, :])
                dg = dgp.next()
                K.actf(dg, gb[0:64, GH * 128:GWD], AF.Exp)
                for h in range(GH):
                    K.stt(Stmp[:, h, :], S[:, h, :], dg[:, h:h + 1], gb[0:64, h * 128:(h + 1) * 128],
                          ALU.mult, ALU.add)
                K.tt(dve, Stmp, Stmp, S, ALU.subtract)
                K.stt(S.rr("p g v -> p (g v)"), Stmp.rr("p g v -> p (g v)"), gate[0:64, r:r + 1],
                      S.rr("p g v -> p (g v)"), ALU.mult, ALU.add)
            K.cp(act, HTb, HT)
            K.cp(act, Sb, S)
            if SLV < 4:
                return
            for ch in range(c.NCHP):
                ssd_chunk(ch * 64, HALOsel if ch == 0 else None, True, False)
                if SLV >= 5:
                    gla_chunk(ch * 64, True, False)
            if SLV < 6:
                return
            write_ssd_state(ssdp[l])
            K.dma(sp, glap[l].rr("(g p) v -> p g v", p=64), S)
            for i in range(NS):
                load_ssd_state(sssd[l, i])
                K.dma(sp, S, sgla[l, i].rr("(g p) v -> p g v", p=64))
                K.cp(act, Sb, S)
                lt0 = TOKP + 64 * i
                ssd_chunk(lt0, sconv[l, i], True, False)
                gla_chunk(lt0, True, False)
                write_ssd_state(ssds[l, i])
                K.dma(sp, glas[l, i].rr("(g p) v -> p g v", p=64), S)

    slopes = [2.0 ** (-8.0 * (h + 1) / DH) for h in range(DH)]

    def phase_attn(l, mixT):
        lam_init = 0.8 - 0.6 * math.exp(-0.3 * (l if layer_idx is None else layer_idx))
        NA = 2 * DH
        nbank = (NA + 2) // 3
        acc_banks = psum.items[8 - nbank:8]
        psr = Pool.__new__(Pool)
        psr.items = psum.items[0:8 - nbank]
        psr.i = 0
        with Scope():
            lamt = K.tile(f"lam{l}", [128, 8], F32)
            l4 = R("lam").rr("p (a d) -> p a d", a=4)
            prod = K.tile(f"lpr{l}", [128, 2, 64], F32)
            K.tt(dve, prod[:, 0, :], l4[:, 0, :], l4[:, 1, :], ALU.mult)
            K.tt(dve, prod[:, 1, :], l4[:, 2, :], l4[:, 3, :], ALU.mult)
            K.red(lamt[:, 0:2], prod)
            K.actf(lamt[:, 2:4], lamt[:, 0:2], AF.Exp)
            K.tt(dve, lamt[:, 4:5], lamt[:, 2:3], lamt[:, 3:4], ALU.subtract)
            K.ts(dve, lamt[:, 5:6], lamt[:, 4:5], lam_init, None, ALU.add)
            lam = lamt[:, 5:6]
            qtp = Pool(K, f"qt{l}_", [128, DH, 128], BF16, 2)
            ktp = Pool(K, f"kt{l}_", [128, DH, 128], BF16, 3)
            vpp = Pool(K, f"vpa{l}_", [128, DH, 132], BF16, 3)
            vsp = Pool(K, f"vs{l}_", [128, DH, 132], BF16, 2)
            ep = Pool(K, f"E{l}_", [128, 2, DH, 128], BF16, 3)
            tmpp = Pool(K, f"tb{l}_", [128, DH, 128], F32, 2)
            accs = Pool(K, f"acs{l}_", [128, NA, 132], F32, 1)
            o0p = Pool(K, f"o0{l}_", [128, DH, 128], F32, 2)
            o1p = Pool(K, f"o1{l}_", [128, DH, 128], F32, 2)
            obp = Pool(K, f"ob{l}_", [128, DH, 128], BF16, 2)
            smp = Pool(K, f"sma{l}_", [128, 16], F32, 3)
            ckp = Pool(K, f"ckf{l}_", [128, DH * 128], F32, 2)
            ckb = Pool(K, f"ckb{l}_", [128, DH * 128], BF16, 2)
            for vp in vpp.items:
                K.memset(dve, vp[:, :, 128:129], 1.0)
                K.memset(dve, vp[:, :, 129:132], 0.0)
            QTv = QTd.rr("(h p) t -> p h t", p=128)
            KTinv = KTin.rr("(h p) t -> p h t", p=128)
            KTSv = KTS.rr("(h p) t -> p h t", p=128)
            KTallv = KTall.rr("(r h p) t -> p r h t", p=128, h=DH)

            ALV = int(os.environ.get("ATT_LEVEL", "9"))

            def attend(lt0, nq, ktiles):
                qt = qtp.next()
                K.dma(sp, qt[:, :, 0:nq], QTv[:, :, lt0:lt0 + nq])
                nkt = len(ktiles)
                for ti, (loader, nk, kind, widx) in enumerate(ktiles):
                    kt, vp = loader()
                    if ALV < 1:
                        continue
                    if kind == "w":
                        vs = vsp.next()
                        for h in range(DH):
                            K.ts(dve, vs[0:nk, h, :], vp[0:nk, h, :], wtab[0:nk, widx * DH + h:widx * DH + h + 1], None,
                                 ALU.mult)
                        vuse = vs
                    else:
                        vuse = vp
                    if ALV < 2:
                        continue
                    E = ep.next()
                    for cm in range(2):
                        psS = psr.next()
                        for h in range(DH):
                            K.mm(psS[0:nk, h * nq:(h + 1) * nq],
                                 kt[cm * 64:(cm + 1) * 64, h, 0:nk], qt[cm * 64:(cm + 1) * 64, h, 0:nq])
                        if ALV < 3:
                            continue
                        ps3 = psS[0:nk, 0:DH * nq].rr("p (h q) -> p h q", q=nq)
                        if kind == "diag":
                            tb = tmpp.next()
                            K.stt(tb[0:nk, :, 0:nq], ps3, 0.125,
                                  bdiag[0:nk, :].rr("p (h q) -> p h q", h=DH)[:, :, 0:nq], ALU.mult, ALU.add)
                            K.actf(E[0:nk, cm, :, 0:nq], tb[0:nk, :, 0:nq], AF.Exp)
                        else:
                            K.actf(E[0:nk, cm, :, 0:nq], ps3, AF.Exp, scale=0.125)
                    if ALV < 4:
                        continue
                    for h in range(DH):
                        for cm in range(2):
                            a = h * 2 + cm
                            bank = acc_banks[a // 3]
                            col = (a % 3) * 132
                            K.mm(bank[0:nq, col:col + 129], E[0:nk, cm, h, 0:nq], vuse[0:nk, h, 0:129],
                                 start=(ti == 0 and a % 3 == 0), stop=(ti == nkt - 1))
                if ALV < 5:
                    return
                ac = accs.next()
                for b in range(nbank):
                    na = min(3, NA - 3 * b)
                    K.cp(act if b % 2 == 0 else dve, ac[0:nq, 3 * b:3 * b + na, :],
                         acc_banks[b][0:nq, 0:na * 132].rr("p (a v) -> p a v", v=132))
                sm = smp.next()
                rl = sm[0:nq, 0:NA]
                K.recip(rl, ac[0:nq, :, 128])
                rl3 = rl.rr("p (h c) -> p h c", c=2)
                K.ts(dve, rl3[:, :, 1], rl3[:, :, 1], lam[0:nq], None, ALU.mult)
                ac4 = ac.rr("p (h c) v -> p h c v", c=2)
                o0 = o0p.next()
                o1 = o1p.next()
                K.tt(dve, o1[0:nq], ac4[0:nq, :, 1, 0:128], rl3[:, :, 1].unsq(2).bc([nq, DH, 128]), ALU.mult)
                K.tt(dve, o0[0:nq], ac4[0:nq, :, 0, 0:128], rl3[:, :, 0].unsq(2).bc([nq, DH, 128]), ALU.mult)
                K.tt(dve, o0[0:nq], o0[0:nq], o1[0:nq], ALU.subtract)
                K.actf(o1[0:nq], o0[0:nq], AF.Square)
                ss = smp.next()[0:nq, 0:DH]
                K.red(ss, o1[0:nq])
                K.rsqrt(ss, ss, 1.0 / 128, post=1.0 - lam_init)
                K.tt(dve, o0[0:nq], o0[0:nq], ss.unsq(2).bc([nq, DH, 128]), ALU.mult)
                ob = obp.next()
                K.tt(dve, ob[0:nq], o0[0:nq], R("og")[0:nq].unsq(1).bc([nq, DH, 128]), ALU.mult)
                psT = psr.next()
                psTb = psT.bitcast(BF16)
                for h in range(DH):
                    K.tr(psTb[:, h * nq:(h + 1) * nq], ob[0:nq, h, :], identb[0:nq, 0:nq])
                mix_store(KC_S, DH, lt0, nq, psTb[:, 0:DH * nq].rr("p (h t) -> p h t", t=nq))

            def dram_loader(ksrc, vsrc, nk):
                def f():
                    kt = ktp.next()
                    vp = vpp.next()
                    K.dma(sp, kt[:, :, 0:nk], ksrc)
                    K.dma(sp, vp[0:nk, :, :], vsrc)
                    return kt, vp
                return f

            def cache_loader(s, m):
                def f():
                    kf = ckp.next()
                    K.dma(sp, kf, ck[l, s, m * 128:(m + 1) * 128, :])
                    kb = ckb.next()
                    K.cp(dve, kb, kf)
                    psk = psr.next()
                    pskb = psk.bitcast(BF16)
                    for h in range(DH):
                        K.tr(pskb[:, h * 128:(h + 1) * 128], kb[:, h * 128:(h + 1) * 128], identb)
                    kt = ktp.next()
                    K.cp(dve, kt, pskb[:, 0:DH * 128].rr("p (h t) -> p h t", h=DH))
                    vf = ckp.next()
                    K.dma(sp, vf, cv[l, s, m * 128:(m + 1) * 128, :])
                    vp = vpp.next()
                    K.cp(act, vp[:, :, 0:128], vf.rr("p (h v) -> p h v", h=DH))
                    return kt, vp
                return f

            NKT = c.NKT
            for j in range(c.NTP):
                tiles = []
                for r in range(NCORE - 1):
                    for i in range(NKT):
                        widx = n_own + r * (2 * NKT - 1) + (i - j + NKT - 1)
                        tiles.append((dram_loader(KTallv[:, r, :, i * 128:(i + 1) * 128],
                                                  VPall[r * TOKP + i * 128:r * TOKP + (i + 1) * 128, :].rr(
                                                      "t (h v) -> t h v", h=DH), 128), 128, "w", widx))
                for i in range(j):
                    tiles.append((dram_loader(KTinv[:, :, i * 128:(i + 1) * 128],
                                              VPin[i * 128:(i + 1) * 128, :].rr("t (h v) -> t h v", h=DH), 128),
                                  128, "w", j - i))
                tiles.append((dram_loader(KTinv[:, :, j * 128:(j + 1) * 128],
                                          VPin[j * 128:(j + 1) * 128, :].rr("t (h v) -> t h v", h=DH), 128),
                              128, "diag", 0))
                attend(j * 128, 128, tiles)
            for s in range(NS):
                tiles = []
                for m in range(c.NPT):
                    tiles.append((cache_loader(s, m), 128, "w", n_own + n_rem + m))
                tiles.append((dram_loader(KTSv[:, :, s * 64:(s + 1) * 64],
                                          VPS[s * 64:(s + 1) * 64, :].rr("t (h v) -> t h v", h=DH), 64),
                              64, "diag", 0))
                attend(TOKP + s * 64, 64, tiles)

    def phase_wout(l, mixT):
        xsrc = xin if (l == 0 or part) else X1
        with Scope():
            wo = K.tile(f"wo{l}", [128, KC, D], BF16)
            mtp = Pool(K, f"mt{l}_", [128, KC, 128], BF16, 2)
            xbp = Pool(K, f"xb{l}_", [128, D], F32, 2)
            wv = w_out[l].rr("(k p) c -> p k c", p=128)
            for cb in range(D // 512):
                K.dma(pool, wo[:, :, cb * 512:(cb + 1) * 512], wv[:, :, cb * 512:(cb + 1) * 512])
            for tt in range(NT):
                r0, r1 = tt * 128, (tt + 1) * 128
                mt = mtp.next()
                K.dma(sp, mt, MIXTv[:, :, r0:r1])
                xb = xbp.next()
                K.dma(sp, xb, xsrc[r0:r1, :])
                for cb in range(D // 512):
                    ps = psum.next()
                    for kc in range(KC):
                        K.mm(ps, mt[:, kc, :], wo[:, kc, cb * 512:(cb + 1) * 512], start=(kc == 0), stop=(kc == KC - 1))
                    K.tt(dve, xb[:, cb * 512:(cb + 1) * 512], xb[:, cb * 512:(cb + 1) * 512], ps, ALU.add)
                K.dma(sp, XM[r0:r1, :], xb)

    def phase_mlp(l):
        xdst = yout if (l == L - 1 or part) else X1
        TS = 768 if NTOK % 768 == 0 else (256 if NTOK % 256 == 0 else 128)
        NF = c.DFF // 128
        K.dma(sp, gT, g2T[l])
        with Scope():
            h2T = K.tile(f"h2T{l}", [128, KC, TS], BF16)
            h1T = K.tile(f"h1T{l}", [128, NF, TS], BF16)
            xpool = Pool(K, f"xm{l}_", [128, D], F32, 1)
            xnpool = Pool(K, f"xmn{l}_", [128, D], BF16, 1)
            stpool = Pool(K, f"stm{l}_", [128, 4], F32, 3)
            w1p = Pool(K, f"w1{l}_", [128, KC, 256], BF16, 2)
            w2p = Pool(K, f"w2{l}_", [128, NF, 128], BF16, 2)
            ytp = Pool(K, f"yt{l}_", [128, TS], F32, 2)
            xblk = Pool(K, f"xk{l}_", [128, 128], F32, 4)
            relup = Pool(K, f"rl{l}_", [128, 512], F32, 2)
            w1v = w1[l].rr("(k p) c -> p k c", p=128)
            w2v = w2[l].rr("(f p) c -> p f c", p=128)
            blocks = [(b0, min(512, TS - b0)) for b0 in range(0, TS, 512)]
            for st_ in range(NTOK // TS):
                t0 = st_ * TS
                for t in range(TS // 128):
                    xs = xpool.next()
                    K.dma(sp, xs, XM[t0 + t * 128:t0 + (t + 1) * 128, :])
                    rms_tile_to_hT(h2T, xs, xnpool, stpool, t)
                nw1 = c.DFF // 256

                def loadw1(i):
                    wb = w1p.next()
                    K.dma(pool, wb, w1v[:, :, i * 256:(i + 1) * 256])
                    return wb
                nxt1 = loadw1(0)
                for i in range(nw1):
                    wb = nxt1
                    if i + 1 < nw1:
                        nxt1 = loadw1(i + 1)
                    for f4 in range(2):
                        fc = i * 2 + f4
                        for (b0, bn) in blocks:
                            ps = psum.next()
                            for kc in range(KC):
                                K.mm(ps[:, 0:bn], wb[:, kc, f4 * 128:(f4 + 1) * 128], h2T[:, kc, b0:b0 + bn],
                                     start=(kc == 0), stop=(kc == KC - 1))
                            rl_ = relup.next()
                            K.actf(rl_[:, 0:bn], ps[:, 0:bn], AF.Relu)
                            K.tt(dve, h1T[:, fc, b0:b0 + bn], rl_[:, 0:bn], rl_[:, 0:bn], ALU.mult)
                ncc = D // 128

                def loadw2(cc):
                    wb = w2p.next()
                    K.dma(pool, wb, w2v[:, :, cc * 128:(cc + 1) * 128])
                    return wb
                nxt2 = loadw2(0)
                for cc in range(ncc):
                    wb = nxt2
                    if cc + 1 < ncc:
                        nxt2 = loadw2(cc + 1)
                    yt = ytp.next()
                    for (b0, bn) in blocks:
                        ps = psum.next()
                        for fc in range(NF):
                            K.mm(ps[:, 0:bn], wb[:, fc, :], h1T[:, fc, b0:b0 + bn], start=(fc == 0), stop=(fc == NF - 1))
                        K.cp(act, yt[:, b0:b0 + bn], ps[:, 0:bn])
                    for t4 in range(0, TS // 128, 4):
                        nt_ = min(4, TS // 128 - t4)
                        ps = psum.next()
                        for t in range(nt_):
                            K.tr(ps[:, t * 128:(t + 1) * 128], yt[:, (t4 + t) * 128:(t4 + t + 1) * 128], identf)
                        for t in range(nt_):
                            r0 = t0 + (t4 + t) * 128
                            xk = xblk.next()
                            K.dma(sp, xk, XM[r0:r0 + 128, cc * 128:(cc + 1) * 128])
                            K.tt(dve, xk, xk, ps[:, t * 128:(t + 1) * 128], ALU.add)
                            K.dma(sp, xdst[r0:r0 + 128, cc * 128:(cc + 1) * 128], xk)

    cur = {}

    class _Stop(Exception):
        pass

    def chk(name):
        if stop_after == name:
            raise _Stop()
    try:
        for l in range(L):
            if part in (0, 1):
                phase_A(l)
                chk("A")
            with Scope():
                rows_t = K.tile(f"rows{l}", [128, RW], F32)
                cur["rows"] = rows_t
                K.dma(sp, rows_t, rowp[l:l + 1, :].bc([128, RW]))
                mixT = None
                if part in (0, 1):
                    phase_A3(l)
                    chk("A3")
                if part in (0, 2, 3):
                    phase_gather1(l)
                    chk("G1")
                if part in (0, 3):
                    phase_attn(l, mixT)
                    chk("ATT")
                if part == 2:
                    phase_scan(l, mixT, mode="p1")
                if part in (0, 3):
                    phase_scan(l, mixT, mode=("p2" if part else None))
                    chk("SCAN")
                    phase_wout(l, mixT)
                    chk("WOUT")
            if part in (0, 3):
                phase_mlp(l)
                chk("MLP")
    except _Stop:
        pass
    K.finish()
    return nc, K


def host_tables(c, core):
    DH, NKT, TOKP = c.DH, c.NKT, c.TOKP
    n_own, n_rem, n_cache, NW = wlin_layout(c)
    slopes = np.array([2.0 ** (-8.0 * (h + 1) / DH) for h in range(DH)], np.float64)
    p = np.arange(128, dtype=np.float64)[:, None]
    wlin = np.zeros((128, NW, DH), np.float64)
    for m in range(n_own):
        wlin[:, m, :] = slopes[None, :] * (p - 128.0 * m)
    for r in range(c.NCORE - 1):
        for d in range(2 * NKT - 1):
            dij = d - (NKT - 1)
            idx = n_own + r * (2 * NKT - 1) + d
            if r < core:
                wlin[:, idx, :] = slopes[None, :] * (p + 128.0 * dij + TOKP * (r - core))
            else:
                wlin[:, idx, :] = NEG
    for m in range(n_cache):
        wlin[:, n_own + n_rem + m, :] = slopes[None, :] * (m * 128.0 + p - c.PAST)
    wlin = np.maximum(wlin, NEG)
    k = np.arange(128)[:, None]
    q = np.arange(128)[None, :]
    allowed = (k // 64) <= (q // 64)
    bd = np.zeros((128, DH, 128), np.float64)
    for h in range(DH):
        bd[:, h, :] = np.where(allowed, -slopes[h] * np.abs(q - k) + slopes[h] * q, NEG)
    gate = np.zeros((128, 8), np.float32)
    gate[:, :core] = 1.0
    sel = np.zeros((128, 8), np.float32)
    if core >= 1:
        sel[:, core - 1] = 1.0
    return (wlin.reshape(128, NW * DH).astype(np.float32), bd.reshape(128, DH * 128).astype(np.float32), gate, sel)


def host_consts():
    i = np.arange(128)
    ident = np.eye(128, dtype=np.float32)
    tri = (i[:, None] <= i[None, :]).astype(np.float32)
    mstr = (i[:, None] > i[None, :]).astype(np.float32)
    ones = np.ones((128, 128), np.float32)
    return np.concatenate([ident, tri, mstr, ones], axis=1)


def make_in_maps(c, inp):
    f = lambda a: np.ascontiguousarray(np.asarray(a, dtype=np.float32))
    L, NC = c.DEPTH, c.NCORE
    lay, RW = row_layout(c)
    rowp = np.zeros((L, RW), np.float32)

    def put(key, arr):
        o, n = lay[key]
        rowp[:, o:o + n] = f(arr).reshape(L, n)
    put("cw", inp["ssd_conv_w"])
    put("cb", inp["ssd_conv_b"])
    put("dtb", inp["ssd_dt_bias"])
    put("alog", inp["ssd_a_log"])
    put("sd", inp["ssd_d"])
    put("sng", inp["ssd_norm_g"])
    put("qg", inp["diff_qn_g"])
    put("kg", inp["diff_kn_g"])
    put("lam", inp["diff_lambda"])
    put("og", inp["diff_out_g"])
    put("ba", inp["gla_ba"])
    put("gng", inp["gla_norm_g"])
    g1T = f(inp["norm1_g"]).reshape(L, c.KC, 128).transpose(0, 2, 1).copy()
    g2T = f(inp["norm2_g"]).reshape(L, c.KC, 128).transpose(0, 2, 1).copy()
    shared = dict(g1T=g1T, g2T=g2T, w_in=f(inp["w_in"]), w_out=f(inp["w_out"]), w1=f(inp["w_mlp1"]),
                  w2=f(inp["w_mlp2"]), rowp=rowp, wa2=f(inp["gla_wa2"]), consts=host_consts())
    xp = f(inp["x_prompt"])[0]
    xs = f(inp["x_sample"])
    ckk = f(inp["cache_diff_k"])
    cvv = f(inp["cache_diff_v"])
    sconv = f(inp["state_ssd_conv"])
    sssd = f(inp["state_ssd"])
    sgla = f(inp["state_gla"])
    maps = []
    for core in range(NC):
        b0, b1 = core * c.NS, (core + 1) * c.NS
        xin = np.concatenate([xp[core * c.TOKP:(core + 1) * c.TOKP], xs[b0:b1].reshape(c.NS * 64, c.D)], axis=0)
        wlin, bd, gate, sel = host_tables(c, core)
        m = dict(shared)
        m.update(xin=np.ascontiguousarray(xin),
                 ck=np.ascontiguousarray(ckk[:, b0:b1].reshape(L, c.NS, c.PAST, c.DH * 128)),
                 cv=np.ascontiguousarray(cvv[:, b0:b1].reshape(L, c.NS, c.PAST, c.DH * 128)),
                 sconv=np.ascontiguousarray(sconv[:, b0:b1]),
                 sssd=np.ascontiguousarray(sssd[:, b0:b1].reshape(L, c.NS, c.SH * 64, 128)),
                 sgla=np.ascontiguousarray(sgla[:, b0:b1].reshape(L, c.NS, c.GH * 64, 128)),
                 gate=gate, sel=sel, wlin=wlin, bdiag=bd)
        maps.append(m)
    return maps


def assemble(c, res):
    L, NC, NS, DH = c.DEPTH, c.NCORE, c.NS, c.DH
    cat = np.concatenate
    yp = cat([r["yout"][:c.TOKP] for r in res], 0)[None]
    ys = cat([r["yout"][c.TOKP:].reshape(NS, 64, c.D) for r in res], 0)
    kp = cat([r["kout"][:, :c.TOKP] for r in res], 1).reshape(L, 1, c.SEQ, DH, 128)
    vp = cat([r["vout"][:, :c.TOKP] for r in res], 1).reshape(L, 1, c.SEQ, DH, 128)
    ks = cat([r["kout"][:, c.TOKP:].reshape(L, NS, 64, DH, 128) for r in res], 1)
    vs = cat([r["vout"][:, c.TOKP:].reshape(L, NS, 64, DH, 128) for r in res], 1)
    last = res[NC - 1]
    cp = last["convp"].reshape(L, 1, 3, c.CONVD)
    hp = last["ssdp"].reshape(L, 1, c.SH, 64, 128)
    sp_ = last["glap"].reshape(L, 1, c.GH, 64, 128)
    cs = cat([r["convs"] for r in res], 1)
    hs = cat([r["ssds"].reshape(L, NS, c.SH, 64, 128) for r in res], 1)
    ss = cat([r["glas"].reshape(L, NS, c.GH, 64, 128) for r in res], 1)
    outs = (yp, ys, kp, vp, cp, hp, sp_, ks, vs, cs, hs, ss)
    return tuple(np.ascontiguousarray(o, dtype=np.float32) for o in outs)


_CACHE = {}


def run_multi(c, inputs):
    NC, L = c.NCORE, c.DEPTH
    c1 = make_cfg(D=c.D, SEQ=c.SEQ, DEC_B=c.DEC_B, DEC_S=c.DEC_S, PAST=c.PAST, DEPTH=1, NCORE=NC)
    maps = make_in_maps(c, inputs)
    cores = list(range(NC))

    def prog(p, l):
        key = (c.D, c.SEQ, c.DEC_B, c.PAST, p, l if p == 3 else 0)
        if key not in _CACHE:
            _CACHE[key] = build_program(c1, part=p, layer_idx=l)[0]
        return _CACHE[key]

    def run(nc, ms):
        return run_bass_kernel_spmd(nc, ms, core_ids=cores).results

    def sl(a, l):
        return np.ascontiguousarray(a[l:l + 1])
    cur_x = [m["xin"] for m in maps]
    per_layer = []
    for l in range(L):
        m1 = [dict(xin=cur_x[i], g1T=sl(maps[i]["g1T"], l), w_in=sl(maps[i]["w_in"], l), rowp=sl(maps[i]["rowp"], l),
                   consts=maps[i]["consts"]) for i in cores]
        r1 = run(prog(1, l), m1)
        KTall = np.concatenate([r1[i]["KTin"] for i in cores], 0)
        VPall = np.concatenate([r1[i]["VPin"] for i in cores], 0)
        HALOall = np.concatenate([r1[i]["HALOin"] for i in cores], 0)
        m2 = [dict(Pscr=r1[i]["Pscr"], HALOall=HALOall, rowp=sl(maps[i]["rowp"], l), consts=maps[i]["consts"],
                   sel=maps[i]["sel"], wa2=sl(maps[i]["wa2"], l)) for i in cores]
        r2 = run(prog(2, l), m2)
        st = np.stack([r2[i]["STin"] for i in cores], 0)
        W = st.shape[2]
        STall = np.ascontiguousarray(st.reshape(NC, 128 // 16, 16, W).transpose(1, 0, 2, 3).reshape(NC * 128, W))
        GSall = np.concatenate([r2[i]["GSin"] for i in cores], 0)
        m3 = []
        for i in cores:
            m = dict(Pscr=r1[i]["Pscr"], QTd=r1[i]["QTd"], KTin=r1[i]["KTin"], VPin=r1[i]["VPin"], KTS=r1[i]["KTS"],
                     VPS=r1[i]["VPS"], KTall=KTall, VPall=VPall, HALOall=HALOall, STall=STall, GSall=GSall,
                     xin=cur_x[i], consts=maps[i]["consts"], gate=maps[i]["gate"], sel=maps[i]["sel"],
                     wlin=maps[i]["wlin"], bdiag=maps[i]["bdiag"])
            for k_ in ("ck", "cv", "sconv", "sssd", "sgla", "g2T", "w_out", "w1", "w2", "rowp", "wa2"):
                m[k_] = sl(maps[i][k_], l)
            m3.append(m)
        r3 = run(prog(3, l), m3)
        cur_x = [np.ascontiguousarray(r3[i]["yout"]) for i in cores]
        per_layer.append((r1, r3))
    res = []
    for i in cores:
        d = dict(yout=cur_x[i])
        for k_ in ("kout", "vout", "convp", "convs"):
            d[k_] = np.concatenate([per_layer[l][0][i][k_] for l in range(L)], 0)
        for k_ in ("ssdp", "ssds", "glap", "glas"):
            d[k_] = np.concatenate([per_layer[l][1][i][k_] for l in range(L)], 0)
        res.append(d)
    return assemble(c, res)


def run_fused(c, inputs):
    key = ("fused", c.D, c.SEQ, c.DEC_B, c.PAST)
    if key not in _CACHE:
        _CACHE[key] = build_program(c)[0]
    maps = make_in_maps(c, inputs)
    res = run_bass_kernel_spmd(_CACHE[key], maps, core_ids=list(range(c.NCORE)))
    return assemble(c, res.results)


FUSED = False


def kernel(**inputs):
    c = make_cfg()
    if FUSED:
        return run_fused(c, inputs)
    return run_multi(c, inputs)
```
